# Optimizing a Trainium2 kernel written in Bass

```python
import math
import jax, jax.numpy as jnp
from jax import lax
import numpy as np

D_MODEL = 4096
BATCH = 1
SEQ = 16384
DEPTH = 1
DEC_BATCH = 4
DEC_SEQ = 4096
PAST_LEN = 128

N_META = 16
GRID_W = 64
ATTN_WIDTH = D_MODEL // 2
HEAD_DIM = 128
N_Q_HEADS = ATTN_WIDTH // HEAD_DIM
N_KV_HEADS = N_Q_HEADS // 4
Q_PER_KV = N_Q_HEADS // N_KV_HEADS
KV_WIDTH = N_KV_HEADS * HEAD_DIM
SSM_WIDTH = D_MODEL - ATTN_WIDTH
SSM_GROUP = 16
N_SSM_GROUPS = SSM_WIDTH // SSM_GROUP
SSM_STATE = 64
MIX_WIDTH = ATTN_WIDTH + SSM_WIDTH
IN_WIDTH = 2 * ATTN_WIDTH + 2 * KV_WIDTH + 2 * SSM_WIDTH
SPLITS = (ATTN_WIDTH,
          ATTN_WIDTH + KV_WIDTH,
          ATTN_WIDTH + 2 * KV_WIDTH,
          2 * ATTN_WIDTH + 2 * KV_WIDTH,
          2 * ATTN_WIDTH + 2 * KV_WIDTH + SSM_WIDTH)
Q_BLOCK = 128
ROPE_THETA = 10000.0
ROPE_FREQS = HEAD_DIM // 4
EPS = 1e-6
DT_MIN = 1e-3
DT_MAX = 1e-1

kernel_name = 'hymba_attn_s5_bidir_encoder'


def rms_norm_f32(x, w):
    xf = x.astype(jnp.float32)
    y = xf * lax.rsqrt(jnp.mean(xf * xf, axis=-1, keepdims=True) + EPS)
    return y * w.astype(jnp.float32)


def rms_norm(x, w):
    return rms_norm_f32(x, w).astype(x.dtype)


def axial_rope(s):
    rows = s // GRID_W
    inv_freq = ROPE_THETA ** (-jnp.arange(ROPE_FREQS, dtype=jnp.float32) / ROPE_FREQS)
    row_ids = jnp.repeat(jnp.arange(rows, dtype=jnp.float32), GRID_W)
    col_ids = jnp.tile(jnp.arange(GRID_W, dtype=jnp.float32), rows)
    ang = jnp.stack([row_ids[:, None] * inv_freq, col_ids[:, None] * inv_freq], axis=1)
    ang = jnp.concatenate([jnp.zeros((N_META, 2, ROPE_FREQS), jnp.float32), ang], axis=0)
    return jnp.cos(ang), jnp.sin(ang)


def apply_rope(x, cos, sin):
    b, l, h, _ = x.shape
    xr = x.reshape(b, l, h, 2, 2, ROPE_FREQS)
    x1, x2 = xr[..., 0, :], xr[..., 1, :]
    c = cos[None, :, None]
    s_ = sin[None, :, None]
    out = jnp.stack([x1 * c - x2 * s_, x1 * s_ + x2 * c], axis=-2)
    return out.reshape(b, l, h, HEAD_DIM)


def attention_branch(q, k, v, q_norm_w, k_norm_w, cos, sin):
    b, l = q.shape[:2]
    q = apply_rope(rms_norm_f32(q, q_norm_w), cos, sin) * (HEAD_DIM ** -0.5)
    k = apply_rope(rms_norm_f32(k, k_norm_w), cos, sin)
    q = q.reshape(b, l, N_KV_HEADS, Q_PER_KV, HEAD_DIM)

    def attend(qb):
        s = jnp.einsum('bqhgd,bkhd->bhgqk', qb, k)
        p = jax.nn.softmax(s, axis=-1).astype(v.dtype)
        return jnp.einsum('bhgqk,bkhd->bqhgd', p, v)

    out_meta = attend(q[:, :N_META]).reshape(b, N_META, ATTN_WIDTH)
    n_real = l - N_META
    n_blk = n_real // Q_BLOCK
    qb = q[:, N_META:].reshape(b, n_blk, Q_BLOCK, N_KV_HEADS, Q_PER_KV, HEAD_DIM)
    qb = qb.transpose(1, 0, 2, 3, 4, 5)
    out_real = lax.map(attend, qb)
    out_real = out_real.transpose(1, 0, 2, 3, 4, 5).reshape(b, n_real, ATTN_WIDTH)
    return jnp.concatenate([out_meta, out_real], axis=1)


def _scan_combine(e_i, e_j):
    a_i, b_i = e_i
    a_j, b_j = e_j
    return a_j * a_i, a_j * b_i + b_j


def s5_direction(u_c, a_re, a_im, log_dt, b_c, c_re, c_im, reverse):
    lam = lax.complex(a_re.astype(jnp.float32), a_im.astype(jnp.float32))
    dt = jnp.exp(log_dt.astype(jnp.float32))[:, None]
    lam_bar = jnp.exp(lam * dt)
    b_bar = ((lam_bar - 1.0) / lam)[..., None] * b_c
    bu = jnp.einsum('blgh,gph->blgp', u_c, b_bar)
    a = jnp.broadcast_to(lam_bar[None, None], (1, bu.shape[1]) + lam_bar.shape)
    _, states = lax.associative_scan(_scan_combine, (a, bu), reverse=reverse, axis=1)
    c = lax.complex(c_re.astype(jnp.float32), c_im.astype(jnp.float32))
    return jnp.real(jnp.einsum('blgp,ghp->blgh', states, c))


def ssm_branch(u, a_re, a_im, log_dt, b_re, b_im, c_re, c_im, d, w_glu, b_glu):
    bsz, l, _ = u.shape
    uf = u.astype(jnp.float32)
    u_c = uf.reshape(bsz, l, N_SSM_GROUPS, SSM_GROUP).astype(jnp.complex64)
    b_c = lax.complex(b_re.astype(jnp.float32), b_im.astype(jnp.float32))
    y = s5_direction(u_c, a_re[0], a_im[0], log_dt[0], b_c, c_re[0], c_im[0], False)
    y = y + s5_direction(u_c, a_re[1], a_im[1], log_dt[1], b_c, c_re[1], c_im[1], True)
    y = y.reshape(bsz, l, SSM_WIDTH) + d.astype(jnp.float32) * uf
    g = jax.nn.gelu(y).astype(u.dtype)
    return g * jax.nn.sigmoid(g @ w_glu + b_glu)


def mixer_layer(h, norm_w, w_in, q_norm_w, k_norm_w, a_re, a_im, log_dt, b_re, b_im,
                c_re, c_im, d, w_glu, b_glu, attn_out_norm_w, ssm_out_norm_w, w_out,
                cos, sin):
    bsz, l, _ = h.shape
    proj = rms_norm(h, norm_w) @ w_in
    q, k, v, z_a, u, z_s = jnp.split(proj, list(SPLITS), axis=-1)
    q = q.reshape(bsz, l, N_Q_HEADS, HEAD_DIM)
    k = k.reshape(bsz, l, N_KV_HEADS, HEAD_DIM)
    v = v.reshape(bsz, l, N_KV_HEADS, HEAD_DIM)
    attn = attention_branch(q, k, v, q_norm_w, k_norm_w, cos, sin)
    attn = rms_norm(attn * jax.nn.silu(z_a), attn_out_norm_w)
    ssm = ssm_branch(u, a_re, a_im, log_dt, b_re, b_im, c_re, c_im, d, w_glu, b_glu)
    ssm = rms_norm(ssm * jax.nn.silu(z_s), ssm_out_norm_w)
    return h + jnp.concatenate([attn, ssm], axis=-1) @ w_out


def encode(x, meta_tokens, norm_w, w_in, q_norm_w, k_norm_w, ssm_a_re, ssm_a_im,
           ssm_log_dt, ssm_b_re, ssm_b_im, ssm_c_re, ssm_c_im, ssm_d, w_glu, b_glu,
           attn_out_norm_w, ssm_out_norm_w, w_out, final_norm_w):
    bsz, s, _ = x.shape
    cos, sin = axial_rope(s)
    meta = jnp.broadcast_to(meta_tokens[None].astype(x.dtype), (bsz, N_META, D_MODEL))
    h = jnp.concatenate([meta, x], axis=1)
    for layer in range(DEPTH):
        h = mixer_layer(h, norm_w[layer], w_in[layer], q_norm_w[layer], k_norm_w[layer],
                        ssm_a_re[layer], ssm_a_im[layer], ssm_log_dt[layer],
                        ssm_b_re[layer], ssm_b_im[layer], ssm_c_re[layer], ssm_c_im[layer],
                        ssm_d[layer], w_glu[layer], b_glu[layer],
                        attn_out_norm_w[layer], ssm_out_norm_w[layer], w_out[layer],
                        cos, sin)
    return rms_norm(h, final_norm_w)[:, N_META:]


def setup_inputs(seed: int = 0) -> dict:
    key = jax.random.key(seed)
    ks = jax.random.split(key, 24)
    f32 = jnp.float32
    nrm = lambda k, shape, scale: jax.random.normal(k, shape, f32) * scale
    a_im_base = jnp.pi * jnp.arange(SSM_STATE, dtype=f32)
    return {
        'x_prompt': nrm(ks[0], (BATCH, SEQ, D_MODEL), 1.0),
        'x_sample': nrm(ks[1], (DEC_BATCH, DEC_SEQ, D_MODEL), 1.0),
        'meta_tokens': nrm(ks[2], (N_META, D_MODEL), 1.0),
        'norm_w': 1.0 + nrm(ks[3], (DEPTH, D_MODEL), 0.02),
        'w_in': nrm(ks[4], (DEPTH, D_MODEL, IN_WIDTH), D_MODEL ** -0.5),
        'q_norm_w': 1.0 + nrm(ks[5], (DEPTH, HEAD_DIM), 0.02),
        'k_norm_w': 1.0 + nrm(ks[6], (DEPTH, HEAD_DIM), 0.02),
        'ssm_a_re': -0.5 + nrm(ks[7], (DEPTH, 2, N_SSM_GROUPS, SSM_STATE), 0.01),
        'ssm_a_im': a_im_base + nrm(ks[8], (DEPTH, 2, N_SSM_GROUPS, SSM_STATE), 0.01),
        'ssm_log_dt': jax.random.uniform(ks[9], (DEPTH, 2, N_SSM_GROUPS), f32,
                                         math.log(DT_MIN), math.log(DT_MAX)),
        'ssm_b_re': nrm(ks[10], (DEPTH, N_SSM_GROUPS, SSM_STATE, SSM_GROUP), (2.0 * SSM_GROUP) ** -0.5),
        'ssm_b_im': nrm(ks[11], (DEPTH, N_SSM_GROUPS, SSM_STATE, SSM_GROUP), (2.0 * SSM_GROUP) ** -0.5),
        'ssm_c_re': nrm(ks[12], (DEPTH, 2, N_SSM_GROUPS, SSM_GROUP, SSM_STATE), (2.0 * SSM_STATE) ** -0.5),
        'ssm_c_im': nrm(ks[13], (DEPTH, 2, N_SSM_GROUPS, SSM_GROUP, SSM_STATE), (2.0 * SSM_STATE) ** -0.5),
        'ssm_d': nrm(ks[14], (DEPTH, SSM_WIDTH), 1.0),
        'w_glu': nrm(ks[15], (DEPTH, SSM_WIDTH, SSM_WIDTH), SSM_WIDTH ** -0.5),
        'b_glu': nrm(ks[16], (DEPTH, SSM_WIDTH), 0.01),
        'attn_out_norm_w': 1.0 + nrm(ks[17], (DEPTH, ATTN_WIDTH), 0.02),
        'ssm_out_norm_w': 1.0 + nrm(ks[18], (DEPTH, SSM_WIDTH), 0.02),
        'w_out': nrm(ks[19], (DEPTH, MIX_WIDTH, D_MODEL), MIX_WIDTH ** -0.5),
        'final_norm_w': 1.0 + nrm(ks[20], (D_MODEL,), 0.02),
    }


def reference(x_prompt, x_sample, meta_tokens, norm_w, w_in, q_norm_w, k_norm_w,
              ssm_a_re, ssm_a_im, ssm_log_dt, ssm_b_re, ssm_b_im, ssm_c_re, ssm_c_im,
              ssm_d, w_glu, b_glu, attn_out_norm_w, ssm_out_norm_w, w_out, final_norm_w):
    y_prompt = encode(x_prompt, meta_tokens, norm_w, w_in, q_norm_w, k_norm_w,
                      ssm_a_re, ssm_a_im, ssm_log_dt, ssm_b_re, ssm_b_im, ssm_c_re,
                      ssm_c_im, ssm_d, w_glu, b_glu, attn_out_norm_w, ssm_out_norm_w,
                      w_out, final_norm_w)
    y_sample = encode(x_sample, meta_tokens, norm_w, w_in, q_norm_w, k_norm_w,
                      ssm_a_re, ssm_a_im, ssm_log_dt, ssm_b_re, ssm_b_im, ssm_c_re,
                      ssm_c_im, ssm_d, w_glu, b_glu, attn_out_norm_w, ssm_out_norm_w,
                      w_out, final_norm_w)
    return (y_prompt, y_sample)
```

```python
import os
import contextlib
import numpy as np
import ml_dtypes
import concourse.bass as bass
import concourse.mybir as mybir
from concourse.bass_utils import run_bass_kernel_spmd

F32 = mybir.dt.float32
BF16 = mybir.dt.bfloat16
I32 = mybir.dt.int32
ALU = mybir.AluOpType
AF = mybir.ActivationFunctionType
AX = mybir.AxisListType

NCORE = 8
D = 4096
TL = 4096
NP_ = 2048
NS_ = 512
TLP = 4224
NMETA = 16
WMAIN = 7168
EPS = 1e-6
GS = 4608
TWO_PI = 6.283185307179586


class Res:
    __slots__ = ("name", "h", "w", "r")

    def __init__(self, name, h):
        self.name = name
        self.h = h
        self.w = {}
        self.r = {}

    def __getitem__(self, k):
        return self.h[k]


class Op:
    __slots__ = ("eng", "fn", "waits", "signal", "val", "is_dma", "sem", "inc")

    def __init__(self, eng, fn, is_dma):
        self.eng = eng
        self.fn = fn
        self.waits = []
        self.signal = False
        self.val = None
        self.is_dma = is_dma
        self.sem = None
        self.inc = 1


class Prog:
    ENGS = ("pe", "act", "dve", "pool", "sp")

    def __init__(self, nc, es, n_dma_sems=12):
        self.nc = nc
        self.es = es
        self.ops = {e: [] for e in self.ENGS}
        self.esem = {e: es.enter_context(nc.semaphore("es_" + e)) for e in self.ENGS}
        self.dsems = {}
        self.dcnt = {}
        self.dlast = {}
        self.drr = {}
        for q in ("sp", "act", "pool"):
            self.dsems[q] = [es.enter_context(nc.semaphore(f"ds_{q}{i}")) for i in range(n_dma_sems)]
            self.drr[q] = 0
        self.all_dma = []

    def _dep(self, op, ev):
        if ev is None or ev is op:
            return
        if (not ev.is_dma) and (not op.is_dma) and ev.eng == op.eng == "pe":
            return
        ev.signal = True
        op.waits.append(ev)

    @staticmethod
    def _key(op):
        return ("d", id(op.sem)) if op.is_dma else ("c", op.eng)

    def _track(self, op, reads, writes, wdisj=()):
        for r in reads:
            for ev in r.w.values():
                self._dep(op, ev)
        for w in writes:
            for ev in w.w.values():
                self._dep(op, ev)
            for ev in w.r.values():
                self._dep(op, ev)
        for w in wdisj:
            for ev in w.r.values():
                self._dep(op, ev)
        k = self._key(op)
        for r in reads:
            r.r[k] = op
        for w in writes:
            w.w = {k: op}
            w.r = {}
        for w in wdisj:
            w.w[k] = op

    def op(self, eng, fn, reads=(), writes=(), wdisj=()):
        o = Op(eng, fn, False)
        self._track(o, reads, writes, wdisj)
        self.ops[eng].append(o)
        return o

    def dma(self, q, fn, reads=(), writes=(), wdisj=(), inc=16):
        o = Op(q, fn, True)
        o.signal = True
        o.inc = inc
        i = self.drr[q]
        self.drr[q] = (i + 1) % len(self.dsems[q])
        sem = self.dsems[q][i]
        key = (q, i)
        prev = self.dlast.get(key)
        if prev is not None:
            o.waits.append(prev)
        self.dcnt[key] = self.dcnt.get(key, 0) + inc
        o.sem = sem
        o.val = self.dcnt[key]
        self.dlast[key] = o
        self._track(o, reads, writes, wdisj)
        self.ops[q].append(o)
        self.all_dma.append(o)
        return o

    def coll(self, fn, reads=(), writes=()):
        o = Op("pool", fn, True)
        o.signal = True
        o.inc = None
        o.sem = self.es.enter_context(self.nc.semaphore(f"cc{len(self.all_dma)}"))
        o.val = 1
        self._track(o, reads, writes)
        self.ops["pool"].append(o)
        self.all_dma.append(o)
        return o

    def barrier(self):
        evs = []
        last = {}
        for o in self.all_dma:
            last[id(o.sem)] = o
        evs.extend(last.values())
        for e in self.ENGS:
            for o in reversed(self.ops[e]):
                if not o.is_dma and o.fn is not None:
                    evs.append(o)
                    break
        for e in self.ENGS:
            b = Op(e, None, False)
            for ev in evs:
                if ev.is_dma or ev.eng != e:
                    ev.signal = True
                    b.waits.append(ev)
            self.ops[e].append(b)

    def emit(self):
        nc = self.nc
        fin = Op("sp", None, False)
        last = {}
        for o in self.all_dma:
            last[id(o.sem)] = o
        fin.waits = list(last.values())
        self.ops["sp"].append(fin)
        LIMIT = 30000
        self.esems = {e: [self.esem[e]] for e in self.ENGS}
        for e in self.ENGS:
            c = 0
            si = 0
            for o in self.ops[e]:
                if o.is_dma:
                    continue
                if o.signal:
                    if c >= LIMIT:
                        c = 0
                        si += 1
                        self.esems[e].append(self.es.enter_context(nc.semaphore(f"es_{e}{si}")))
                    c += 1
                    o.val = c
                    o.sem = self.esems[e][si]
        engobj = {"pe": "tensor", "act": "scalar", "dve": "vector", "pool": "gpsimd", "sp": "sync"}
        with nc.Block() as block:
            for e in self.ENGS:
                def body(eng, e=e):
                    waited = {}
                    for o in self.ops[e]:
                        need = {}
                        for ev in o.waits:
                            sem = ev.sem
                            k = id(sem)
                            if need.get(k, (None, 0))[1] < ev.val:
                                need[k] = (sem, ev.val)
                        for k, (sem, val) in need.items():
                            if waited.get(k, 0) >= val:
                                continue
                            waited[k] = val
                            eng.wait_ge(sem, val)
                        if o.fn is None:
                            continue
                        ins = o.fn(eng)
                        if o.is_dma:
                            if o.inc is None:
                                ins.then_inc(o.sem)
                            else:
                                ins.then_inc(o.sem, o.inc)
                        elif o.signal:
                            ins.then_inc(o.sem, 1)
                getattr(block, engobj[e])(body)


ARENA_BYTES = 196 * 1024


class Builder:
    def __init__(self, debug=None):
        self.debug = debug
        self.nc = bass.Bass("TRN2", target_bir_lowering=False)
        self.es = contextlib.ExitStack()
        self.P = Prog(self.nc, self.es)
        self.arena = self.es.enter_context(self.nc.sbuf_tensor("arena", [128, ARENA_BYTES // 2], BF16))
        self.aoff = 0
        self.PS2 = [self.es.enter_context(self.nc.psum_tensor(f"ps2_{i}", [128, 2, 512], F32)) for i in range(4)]

    def din(self, name, shape, dt=F32):
        return Res(name, self.nc.dram_tensor(name, list(shape), dt, kind="ExternalInput").ap())

    def dout(self, name, shape, dt=F32):
        return Res(name, self.nc.dram_tensor(name, list(shape), dt, kind="ExternalOutput").ap())

    def dscr(self, name, shape, dt, dbg=None):
        kind = "ExternalOutput" if (dbg and self.debug and dbg in self.debug.split(",")) else "Internal"
        return Res(name, self.nc.dram_tensor(name, list(shape), dt, kind=kind).ap())

    def sb(self, name, shape, dt):
        return Res(name, self.es.enter_context(self.nc.sbuf_tensor(name, list(shape), dt)))

    def A(self, name, fshape, dt):
        n = int(np.prod(fshape))
        sz = 4 if dt in (F32, I32) else 2
        off = (self.aoff + 63) // 64 * 64
        nb = n * sz
        assert off + nb <= ARENA_BYTES, (name, off, nb)
        self.aoff = off + nb
        v = self.arena[:, off // 2:(off + nb) // 2]
        if sz == 4:
            v = v.bitcast(dt)
        if len(fshape) > 1:
            names = " ".join(f"d{i}" for i in range(len(fshape)))
            v = v.rearrange(f"p ({names}) -> p {names}", **{f"d{i}": int(fshape[i]) for i in range(len(fshape))})
        return Res(name, v)

    def dump(self, tl_ap, n, reads):
        if self.DBG is None:
            return
        o = self.DBG
        off = getattr(self, "dbgoff", 0)
        self.dbgoff = off + n
        self.P.dma("sp", lambda e: e.dma_start(out=o[:, off:off + n], in_=tl_ap), reads, [], wdisj=[o])

    def new_phase(self):
        self.P.barrier()
        self.aoff = 0

    def bank(self, i, name, dt=F32, fshape=None):
        v = self.PS2[i // 2][:, i % 2, :]
        if dt == BF16:
            v = v.bitcast(BF16)
        if fshape is not None:
            names = " ".join(f"d{j}" for j in range(len(fshape)))
            v = v.rearrange(f"p ({names}) -> p {names}", **{f"d{j}": int(fshape[j]) for j in range(len(fshape))})
        return Res(name, v)

    def build(self):
        nc, P = self.nc, self.P
        dbg = set((self.debug or "").split(","))
        x_loc = self.din("x_loc", [TLP, D])
        cs_loc = self.din("cs_loc", [TLP, 128])
        nwT = self.din("nwT", [128, 32])
        w_main = self.din("w_main", [D, WMAIN])
        w_u = self.din("w_u", [D, 256])
        qkw = self.din("qkw", [2, 128])
        ident_in = self.din("ident_in", [128, 128], BF16)
        self.x_loc = x_loc
        sp_in = {n: self.din(n, shp) for n, shp in (
            ("s_ar", [128, 16]), ("s_ai", [128, 16]), ("s_ldt", [128, 16]), ("s_br", [128, 8, 16]),
            ("s_bi", [128, 8, 16]), ("s_cr", [128, 16, 16]), ("s_ci", [128, 16, 16]), ("s_d", [128, 2]))}
        w_glu = self.din("w_glu", [2048, 2048])
        bgT = self.din("bgT", [128, 16])
        w_out = self.din("w_out", [D, D])
        wnT = self.din("wnT", [128, 32])
        fnw = self.din("fnw", [1, D])
        idxG = self.din("idxG", [128, 16, 8], I32)
        out_loc = self.dout("out_loc", [TL, D])
        self.DBG = self.dout("DBG", [128, 16384]) if "Cdbg" in dbg else None

        WB = self.dscr("WB", [D, WMAIN], BF16)
        WUB = self.dscr("WUB", [D, 256], BF16)
        XNT_loc = [self.dscr(f"XNT_loc{i}", [D // 2, TLP], BF16, dbg="A") for i in range(2)]
        QT_loc = self.dscr("QT_loc", [16 * 128, TLP], BF16, dbg="A")
        KT_loc = self.dscr("KT_loc", [4 * 128, TLP], BF16, dbg="A")
        V_loc = self.dscr("V_loc", [TLP, 512], BF16, dbg="A")
        ZT_loc = self.dscr("ZT_loc", [4096, TLP], BF16, dbg="A")
        XNT_all = [self.dscr(f"XNT_all{i}", [NCORE * D // 2, TLP], BF16) for i in range(2)]
        KT_all = self.dscr("KT_all", [NCORE * 512, TLP], BF16)
        V_all = self.dscr("V_all", [NCORE * TLP, 512], BF16)
        AG_loc = self.dscr("AG_loc", [2048, TLP], BF16, dbg="B")
        UT = self.dscr("UT", [256, NCORE * TLP], BF16, dbg="U")
        G_loc = self.dscr("G_loc", [256, NCORE * GS], BF16, dbg="C")
        G_all = self.dscr("G_all", [NCORE * 256, NCORE * GS], BF16)
        WGB = self.dscr("WGB", [2048, 2048], BF16)
        WOB = self.dscr("WOB", [D, D], BF16)
        self.WUB, self.XNT_all = WUB, XNT_all

        ident = self.sb("ident", [128, 128], BF16)
        ones = self.sb("ones", [128, 128], BF16)
        nw = self.sb("nw", [128, 32], F32)
        qkwb = self.sb("qkwb", [128, 2, 128], F32)
        self.ident, self.ones = ident, ones
        P.dma("sp", lambda e: e.dma_start(out=ident[:], in_=ident_in[:, :]), [ident_in], [ident])
        P.dma("sp", lambda e: e.dma_start(out=nw[:], in_=nwT[:, :]), [nwT], [nw])
        P.dma("sp", lambda e: e.dma_start(out=qkwb[:, 0, :], in_=qkw[0:1, :].partition_broadcast(128)), [qkw], [qkwb])
        P.dma("sp", lambda e: e.dma_start(out=qkwb[:, 1, :], in_=qkw[1:2, :].partition_broadcast(128)), [qkw], [qkwb])
        P.op("dve", lambda e: e.memset(ones[:], 1.0), [], [ones])
        P.op("dve", lambda e: e.tensor_scalar(out=qkwb[:, 0, :], in0=qkwb[:, 0, :], scalar1=float(128 ** -0.5),
                                              scalar2=None, op0=ALU.mult), [qkwb], [qkwb])

        for r0 in range(0, D, 256):
            P.dma("pool", lambda e, r0=r0: e.dma_start(out=WB[r0:r0 + 256, :], in_=w_main[r0:r0 + 256, :]),
                  [w_main], [], wdisj=[WB])
        P.dma("pool", lambda e: e.dma_start(out=WUB[:, :], in_=w_u[:, :]), [w_u], [WUB])

        if "skipA" not in dbg:
            self.phase_a(x_loc, cs_loc, nw, qkwb, ident, WB, XNT_loc, QT_loc, KT_loc, V_loc, ZT_loc)
        if "stopA" in dbg:
            P.emit()
            return nc
        rg = [list(range(NCORE))]
        P.coll(lambda e: e.collective_compute("AllGather", ALU.bypass, replica_groups=rg,
                                              ins=[KT_loc[:, :]], outs=[KT_all[:, :]]), [KT_loc], [KT_all])
        P.coll(lambda e: e.collective_compute("AllGather", ALU.bypass, replica_groups=rg,
                                              ins=[V_loc[:, :]], outs=[V_all[:, :]]), [V_loc], [V_all])
        for i in range(2):
            P.coll(lambda e, i=i: e.collective_compute("AllGather", ALU.bypass, replica_groups=rg,
                                                       ins=[XNT_loc[i][:, :]], outs=[XNT_all[i][:, :]]),
                   [XNT_loc[i]], [XNT_all[i]])
        self.new_phase()
        if "skipB" not in dbg:
            self.phase_b(QT_loc, KT_all, V_all, ZT_loc, AG_loc)
            self.new_phase()
        if "stopB" in dbg:
            P.emit()
            return nc
        if "skipA" not in dbg:
            self.phase_u(WUB, XNT_all, UT)
            self.new_phase()
        if "stopU" in dbg:
            P.emit()
            return nc
        self.phase_c(sp_in, UT, G_loc)
        if "stopC" in dbg:
            P.emit()
            return nc
        P.coll(lambda e: e.collective_compute("AllGather", ALU.bypass, replica_groups=rg,
                                              ins=[G_loc[:, :]], outs=[G_all[:, :]]), [G_loc], [G_all])
        self.new_phase()
        self.phase_e(x_loc, w_glu, bgT, w_out, wnT, fnw, idxG, WGB, WOB, G_all, AG_loc, ZT_loc, out_loc)
        P.emit()
        return nc

    def phase_a(self, x_loc, cs_loc, nw, qkwb, ident, WB, XNT_loc, QT_loc, KT_loc, V_loc, ZT_loc):
        P = self.P
        xt = self.A("a_x", [D], F32)
        xb = [self.A(f"a_xb{i}", [D], BF16) for i in range(2)]
        xnT = self.A("a_xnT", [32, 1024], BF16)
        Wc = [self.A(f"a_W{i}", [32, 512], BF16) for i in range(2)]
        ss = self.A("a_ss", [8], F32)
        rs = self.A("a_rs", [8], F32)
        cs = self.A("a_cs", [8, 128], F32)
        qn = self.A("a_qn", [512], F32)
        qa = self.A("a_qa", [512], F32)
        qb = self.A("a_qb", [512], F32)
        qr = self.A("a_qr", [512], BF16)
        junk = self.A("a_junk", [512], BF16)
        stg = self.A("a_stg", [4, 1024], BF16)
        vst = [self.A(f"a_vst{i}", [512], BF16) for i in range(2)]
        zsb = [self.A(f"a_zsb{i}", [512], BF16) for i in range(2)]
        psT = [self.bank(i, f"a_psT{i}", BF16, [8, 128]) for i in range(2)]
        psM = [self.bank(2 + i, f"a_psM{i}") for i in range(3)]
        psQ = [self.bank(5, "a_psQ0", BF16, [8, 128])]

        blocks = [(0, 8), (8, 8), (16, 8), (24, 8), (32, 1)]
        wi = 0
        mi = 0
        ti = 0
        zi = 0
        vi = 0
        for (t0, nt) in blocks:
            tok0 = t0 * 128
            ntok = nt * 128
            P.dma("sp", lambda e, tok0=tok0, nt=nt: e.dma_start(
                out=cs[:, 0:nt, :], in_=cs_loc[tok0:tok0 + nt * 128, :].rearrange("(t p) c -> p t c", p=128)),
                [cs_loc], [cs])
            for i in range(nt):
                r0 = tok0 + i * 128
                xbi = xb[i % 2]
                P.dma("sp", lambda e, r0=r0: e.dma_start(out=xt[:], in_=x_loc[r0:r0 + 128, :]), [x_loc], [xt])
                P.op("dve", lambda e: e.memset(ss[:, 0:1], 0.0), [], [ss])
                P.op("act", lambda e, xbi=xbi: e.activation(out=xbi[:], in_=xt[:], func=AF.Square,
                                                            accum_out=ss[:, 0:1]), [xt, ss], [xbi, ss])
                P.op("dve", lambda e: e.tensor_scalar(out=rs[:, 0:1], in0=ss[:, 0:1], scalar1=1.0 / D, scalar2=EPS,
                                                      op0=ALU.mult, op1=ALU.add), [ss], [rs])
                P.op("act", lambda e: e.activation(out=rs[:, 0:1], in_=rs[:, 0:1], func=AF.Sqrt), [rs], [rs])
                P.op("dve", lambda e: e.reciprocal(out=rs[:, 0:1], in_=rs[:, 0:1]), [rs], [rs])
                P.op("act", lambda e, xbi=xbi: e.activation(out=xbi[:], in_=xt[:], func=AF.Copy,
                                                            scale=rs[:, 0:1]), [xt, rs], [xbi])
                for g in range(4):
                    pt = psT[ti % 2]
                    ti += 1
                    for kk in range(8):
                        k = g * 8 + kk
                        P.op("pe", lambda e, pt=pt, kk=kk, k=k, xbi=xbi: e.transpose(
                            out=pt[:, kk, :], in_=xbi[:, k * 128:(k + 1) * 128], identity=ident[:]),
                            [xbi, ident], [pt])
                    P.op("dve", lambda e, pt=pt, g=g, i=i: e.tensor_tensor(
                        out=xnT[:, g * 8:(g + 1) * 8, i * 128:(i + 1) * 128], in0=pt[:, :, :],
                        in1=nw[:, g * 8:(g + 1) * 8].unsqueeze(2).to_broadcast([128, 8, 128]), op=ALU.mult),
                        [pt, nw], [xnT])
            for hh in range(2):
                P.dma("sp", lambda e, tok0=tok0, ntok=ntok, hh=hh: e.dma_start(
                    out=XNT_loc[hh][:, tok0:tok0 + ntok].rearrange("(k p) t -> p k t", p=128),
                    in_=xnT[:, hh * 16:(hh + 1) * 16, 0:ntok]), [xnT], [], wdisj=[XNT_loc[hh]])
            for ch in range(14):
                W = Wc[wi % 2]
                wi += 1
                c0 = ch * 512
                P.dma("sp", lambda e, W=W, c0=c0: e.dma_start(
                    out=W[:], in_=WB[:, c0:c0 + 512].rearrange("(k p) c -> p k c", p=128)), [WB], [W])
                if ch < 6:
                    for i in range(nt):
                        pm = psM[mi % 3]
                        mi += 1
                        for k in range(32):
                            P.op("pe", lambda e, pm=pm, k=k, i=i, W=W: e.matmul(
                                pm[:], lhsT=xnT[:, k, i * 128:(i + 1) * 128], rhs=W[:, k, :],
                                start=(k == 0), stop=(k == 31)), [xnT, W], [pm])
                        if ch == 5:
                            vs = vst[vi % 2]
                            vi += 1
                            P.op("act", lambda e, vs=vs, pm=pm: e.activation(out=vs[:], in_=pm[:], func=AF.Copy),
                                 [pm], [vs])
                            r0 = tok0 + i * 128
                            P.dma("sp", lambda e, vs=vs, r0=r0: e.dma_start(out=V_loc[r0:r0 + 128, :], in_=vs[:]),
                                  [vs], [], wdisj=[V_loc])
                            continue
                        wsel = 0 if ch < 4 else 1
                        P.op("dve", lambda e: e.memset(ss[:, 4:8], 0.0), [], [ss])
                        for h in range(4):
                            P.op("act", lambda e, h=h, pm=pm: e.activation(
                                out=junk[:, h * 128:(h + 1) * 128], in_=pm[:, h * 128:(h + 1) * 128],
                                func=AF.Square, accum_out=ss[:, 4 + h:5 + h]), [pm, ss], [junk, ss])
                        P.op("dve", lambda e: e.tensor_scalar(out=rs[:, 4:8], in0=ss[:, 4:8], scalar1=1.0 / 128,
                                                              scalar2=EPS, op0=ALU.mult, op1=ALU.add), [ss], [rs])
                        P.op("act", lambda e: e.activation(out=rs[:, 4:8], in_=rs[:, 4:8], func=AF.Sqrt), [rs], [rs])
                        P.op("dve", lambda e: e.reciprocal(out=rs[:, 4:8], in_=rs[:, 4:8]), [rs], [rs])
                        for h in range(4):
                            P.op("dve", lambda e, h=h, pm=pm, wsel=wsel: e.scalar_tensor_tensor(
                                out=qn[:, h * 128:(h + 1) * 128], in0=pm[:, h * 128:(h + 1) * 128],
                                scalar=rs[:, 4 + h:5 + h], in1=qkwb[:, wsel, :], op0=ALU.mult, op1=ALU.mult),
                                [pm, rs, qkwb], [qn])
                        cosb = cs[:, i, 0:64].rearrange("p (a f) -> p a f", a=2)
                        sinb = cs[:, i, 64:128].rearrange("p (a f) -> p a f", a=2)
                        for hh in range(4):
                            qv = qn[:, hh * 128:(hh + 1) * 128].rearrange("p (a b f) -> p a b f", a=2, b=2)
                            av = qa[:, hh * 128:(hh + 1) * 128].rearrange("p (a b f) -> p a b f", a=2, b=2)
                            bv = qb[:, hh * 128:(hh + 1) * 128].rearrange("p (a b f) -> p a b f", a=2, b=2)
                            cb = cosb.unsqueeze(2).to_broadcast([128, 2, 2, 32])
                            sb_ = sinb.unsqueeze(2).to_broadcast([128, 2, 2, 32])
                            P.op("dve", lambda e, av=av, qv=qv, cb=cb: e.tensor_tensor(out=av, in0=qv, in1=cb, op=ALU.mult),
                                 [qn, cs], [qa])
                            P.op("pool", lambda e, bv=bv, qv=qv, sb_=sb_: e.tensor_tensor(out=bv, in0=qv, in1=sb_, op=ALU.mult),
                                 [qn, cs], [qb])
                        qrv = qr[:, :].rearrange("p (h a b f) -> p (h a) b f", h=4, a=2, b=2)
                        qav = qa[:, :].rearrange("p (h a b f) -> p (h a) b f", h=4, a=2, b=2)
                        qbv = qb[:, :].rearrange("p (h a b f) -> p (h a) b f", h=4, a=2, b=2)
                        P.op("dve", lambda e, qrv=qrv, qav=qav, qbv=qbv: e.tensor_tensor(
                            out=qrv[:, :, 0, :], in0=qav[:, :, 0, :], in1=qbv[:, :, 1, :], op=ALU.subtract),
                            [qa, qb], [qr])
                        P.op("pool", lambda e, qrv=qrv, qav=qav, qbv=qbv: e.tensor_tensor(
                            out=qrv[:, :, 1, :], in0=qbv[:, :, 0, :], in1=qav[:, :, 1, :], op=ALU.add),
                            [qa, qb], [qr])
                        pq = psQ[0]
                        for h in range(4):
                            P.op("pe", lambda e, pq=pq, h=h: e.transpose(
                                out=pq[:, h, :], in_=qr[:, h * 128:(h + 1) * 128], identity=ident[:]),
                                [qr, ident], [pq])
                        P.op("act", lambda e, pq=pq, i=i: e.activation(
                            out=stg[:, :, i * 128:(i + 1) * 128], in_=pq[:, 0:4, :], func=AF.Copy), [pq], [stg])
                    if ch < 4:
                        P.dma("sp", lambda e, ch=ch, tok0=tok0, ntok=ntok: e.dma_start(
                            out=QT_loc[ch * 512:(ch + 1) * 512, tok0:tok0 + ntok].rearrange("(h d) t -> d h t", d=128),
                            in_=stg[:, :, 0:ntok]), [stg], [], wdisj=[QT_loc])
                    elif ch == 4:
                        P.dma("sp", lambda e, tok0=tok0, ntok=ntok: e.dma_start(
                            out=KT_loc[:, tok0:tok0 + ntok].rearrange("(h d) t -> d h t", d=128),
                            in_=stg[:, :, 0:ntok]), [stg], [], wdisj=[KT_loc])
                else:
                    zrow0 = (ch - 6) * 512
                    for j in range(4):
                        for hf in range((ntok + 511) // 512):
                            n = min(512, ntok - hf * 512)
                            pm = psM[mi % 3]
                            mi += 1
                            for k in range(32):
                                P.op("pe", lambda e, pm=pm, k=k, j=j, hf=hf, n=n, W=W: e.matmul(
                                    pm[:, 0:n], lhsT=W[:, k, j * 128:(j + 1) * 128],
                                    rhs=xnT[:, k, hf * 512:hf * 512 + n], start=(k == 0), stop=(k == 31)),
                                    [xnT, W], [pm])
                            zs = zsb[zi % 2]
                            zi += 1
                            P.op("act", lambda e, zs=zs, pm=pm, n=n: e.activation(
                                out=zs[:, 0:n], in_=pm[:, 0:n], func=AF.Silu), [pm], [zs])
                            P.dma("sp", lambda e, zs=zs, zrow0=zrow0, j=j, tok0=tok0, hf=hf, n=n: e.dma_start(
                                out=ZT_loc[zrow0 + j * 128:zrow0 + (j + 1) * 128, tok0 + hf * 512:tok0 + hf * 512 + n],
                                in_=zs[:, 0:n]), [zs], [], wdisj=[ZT_loc])

    def seqs(self):
        return [(0, NP_)] + [(NP_ + j * NS_, NS_) for j in range(4)]

    def phase_b(self, QT_loc, KT_all, V_all, ZT_loc, AG_loc):
        P = self.P
        ones = self.ones
        NKMAX = 8 * NP_ + NMETA
        KTs = [self.A(f"b_KT{i}", [NKMAX], BF16) for i in range(2)]
        Vs = [self.A(f"b_V{i}", [129, 128], BF16) for i in range(2)]
        Pt = [self.A(f"b_Pt{i}", [2, 512], BF16) for i in range(3)]
        Qs = [self.A(f"b_Q{i}", [512], BF16) for i in range(2)]
        Zs = [self.A(f"b_Z{i}", [512], BF16) for i in range(2)]
        rec = self.A("b_rec", [512], F32)
        tmp = self.A("b_tmp", [512], F32)
        ag = [self.A(f"b_ag{i}", [512], BF16) for i in range(2)]
        S2 = [Res(f"b_S{i}", self.PS2[i]) for i in range(2)]
        Ops = [self.bank(4, "b_O0"), self.bank(6, "b_O1")]
        Dps = [self.bank(5, "b_D0"), self.bank(7, "b_D1")]
        kvi = 0
        qi = 0
        si = 0
        pi = 0
        for hk in range(4):
            for (slot0, nper) in self.seqs():
                KT = KTs[kvi % 2]
                V = Vs[kvi % 2]
                kvi += 1
                nfull = 8 * nper // 128
                tpr = nper // 128
                for r in range(NCORE):
                    P.dma("sp", lambda e, KT=KT, r=r, hk=hk, slot0=slot0, nper=nper: e.dma_start(
                        out=KT[:, r * nper:(r + 1) * nper],
                        in_=KT_all[r * 512 + hk * 128:r * 512 + (hk + 1) * 128, slot0:slot0 + nper]),
                        [KT_all], [], wdisj=[KT])
                    P.dma("act", lambda e, V=V, r=r, hk=hk, slot0=slot0, nper=nper, tpr=tpr: e.dma_start(
                        out=V[:, r * tpr:(r + 1) * tpr, :],
                        in_=V_all[r * TLP + slot0:r * TLP + slot0 + nper, hk * 128:(hk + 1) * 128].rearrange(
                            "(t p) d -> p t d", p=128)), [V_all], [], wdisj=[V])
                P.dma("sp", lambda e, KT=KT, hk=hk, nper=nper: e.dma_start(
                    out=KT[:, 8 * nper:8 * nper + NMETA], in_=KT_all[hk * 128:(hk + 1) * 128, TL:TL + NMETA]),
                    [KT_all], [], wdisj=[KT])
                P.dma("act", lambda e, V=V, hk=hk, nfull=nfull: e.dma_start(
                    out=V[0:NMETA, nfull, :], in_=V_all[TL:TL + NMETA, hk * 128:(hk + 1) * 128]),
                    [V_all], [], wdisj=[V])
                for g in range(4):
                    head = hk * 4 + g
                    for qb in range(nper // 512):
                        q0 = slot0 + qb * 512
                        Q = Qs[qi % 2]
                        Z = Zs[qi % 2]
                        Op_ = Ops[qi % 2]
                        Dp = Dps[qi % 2]
                        agt = ag[qi % 2]
                        qi += 1
                        P.dma("sp", lambda e, Q=Q, head=head, q0=q0: e.dma_start(
                            out=Q[:], in_=QT_loc[head * 128:(head + 1) * 128, q0:q0 + 512]), [QT_loc], [Q])
                        P.dma("sp", lambda e, Z=Z, head=head, q0=q0: e.dma_start(
                            out=Z[:], in_=ZT_loc[head * 128:(head + 1) * 128, q0:q0 + 512]), [ZT_loc], [Z])
                        npair = nfull // 2
                        for kp in range(npair + 1):
                            S = S2[si % 2]
                            si += 1
                            pt = Pt[pi % 3]
                            pi += 1
                            if kp < npair:
                                for a in range(2):
                                    kt = kp * 2 + a
                                    P.op("pe", lambda e, S=S, a=a, kt=kt, KT=KT, Q=Q: e.matmul(
                                        S[:, a, :], lhsT=KT[:, kt * 128:(kt + 1) * 128], rhs=Q[:],
                                        start=True, stop=True), [KT, Q], [S])
                                P.op("act", lambda e, S=S, pt=pt: e.activation(out=pt[:, :, :], in_=S[:, :, :], func=AF.Exp),
                                     [S], [pt])
                                for a in range(2):
                                    kt = kp * 2 + a
                                    P.op("pe", lambda e, Op_=Op_, V=V, kt=kt, pt=pt, a=a: e.matmul(
                                        Op_[:], lhsT=V[:, kt, :], rhs=pt[:, a, :], start=(kt == 0), stop=False),
                                        [V, pt], [Op_])
                                    P.op("pe", lambda e, Dp=Dp, pt=pt, a=a, kt=kt: e.matmul(
                                        Dp[:], lhsT=ones[:, :], rhs=pt[:, a, :], start=(kt == 0), stop=False),
                                        [ones, pt], [Dp])
                            else:
                                P.op("pe", lambda e, S=S, KT=KT, Q=Q, nper=nper: e.matmul(
                                    S[0:NMETA, 0, :], lhsT=KT[:, 8 * nper:8 * nper + NMETA], rhs=Q[:],
                                    start=True, stop=True), [KT, Q], [S])
                                P.op("act", lambda e, S=S, pt=pt: e.activation(
                                    out=pt[0:NMETA, 0, :], in_=S[0:NMETA, 0, :], func=AF.Exp), [S], [pt])
                                P.op("pe", lambda e, Op_=Op_, V=V, nfull=nfull, pt=pt: e.matmul(
                                    Op_[:], lhsT=V[0:NMETA, nfull, :], rhs=pt[0:NMETA, 0, :], start=False, stop=True),
                                    [V, pt], [Op_])
                                P.op("pe", lambda e, Dp=Dp, pt=pt: e.matmul(
                                    Dp[:], lhsT=ones[0:NMETA, :], rhs=pt[0:NMETA, 0, :], start=False, stop=True),
                                    [ones, pt], [Dp])
                        P.op("dve", lambda e, Dp=Dp: e.reciprocal(out=rec[:], in_=Dp[:]), [Dp], [rec])
                        P.op("dve", lambda e, Op_=Op_: e.tensor_tensor(out=tmp[:], in0=Op_[:], in1=rec[:], op=ALU.mult),
                             [Op_, rec], [tmp])
                        P.op("pool", lambda e, agt=agt, Z=Z: e.tensor_tensor(out=agt[:], in0=tmp[:], in1=Z[:], op=ALU.mult),
                             [tmp, Z], [agt])
                        P.dma("sp", lambda e, agt=agt, head=head, q0=q0: e.dma_start(
                            out=AG_loc[head * 128:(head + 1) * 128, q0:q0 + 512], in_=agt[:]), [agt], [], wdisj=[AG_loc])


    def phase_u(self, WUB, XNT_all, UT):
        P = self.P
        WU = self.A("u_W", [32, 256], BF16)
        X = [self.A(f"u_X{i}", [32, 512], BF16) for i in range(2)]
        ust = [self.A(f"u_st{i}", [512], BF16) for i in range(2)]
        psU = [self.bank(i, f"u_ps{i}") for i in range(4)]
        P.dma("sp", lambda e: e.dma_start(out=WU[:], in_=WUB[:, :].rearrange("(k p) c -> p k c", p=128)), [WUB], [WU])
        xi = 0
        ui = 0
        for r in range(NCORE):
            for cc in range(9 if r == 0 else 8):
                col0 = cc * 512
                n = 512 if cc < 8 else 128
                Xt = X[xi % 2]
                xi += 1
                for hh in range(2):
                    P.dma("sp" if hh == 0 else "act", lambda e, Xt=Xt, hh=hh, r=r, col0=col0, n=n: e.dma_start(
                        out=Xt[:, hh * 16:(hh + 1) * 16, 0:n],
                        in_=XNT_all[hh][r * 2048:(r + 1) * 2048, col0:col0 + n].rearrange("(k p) t -> p k t", p=128)),
                        [XNT_all[hh]], [], wdisj=[Xt])
                for ct in range(2):
                    pu = psU[ui % 4]
                    st = ust[ui % 2]
                    ui += 1
                    for k in range(32):
                        P.op("pe", lambda e, pu=pu, k=k, ct=ct, Xt=Xt, n=n: e.matmul(
                            pu[:, 0:n], lhsT=WU[:, k, ct * 128:(ct + 1) * 128], rhs=Xt[:, k, 0:n],
                            start=(k == 0), stop=(k == 31)), [WU, Xt], [pu])
                    P.op("act", lambda e, st=st, pu=pu, n=n: e.activation(out=st[:, 0:n], in_=pu[:, 0:n], func=AF.Copy),
                         [pu], [st])
                    P.dma("sp", lambda e, st=st, ct=ct, r=r, col0=col0, n=n: e.dma_start(
                        out=UT[ct * 128:(ct + 1) * 128, r * TLP + col0:r * TLP + col0 + n], in_=st[:, 0:n]),
                        [st], [], wdisj=[UT])

    def ssm_params(self, sp_in):
        P = self.P
        t = {n: self.sb("p_" + n, shp, F32) for n, shp in (
            ("ar", [128, 16]), ("ai", [128, 16]), ("ldt", [128, 16]), ("br", [128, 8, 16]), ("bi", [128, 8, 16]),
            ("cr", [128, 16, 16]), ("ci", [128, 16, 16]), ("d", [128, 2]))}
        for n in t:
            P.dma("sp", lambda e, n=n: e.dma_start(out=t[n][:], in_=sp_in["s_" + n][:]), [sp_in["s_" + n]], [t[n]])
        rr = self.sb("p_rr", [128, 16], F32)
        th = self.sb("p_th", [128, 16], F32)
        bbr = self.sb("p_bbr", [128, 16, 16], F32)
        bbi = self.sb("p_bbi", [128, 16, 16], F32)
        jf = self.sb("p_jf", [128, 513], F32)
        ji = self.A("p_ji", [513], I32)
        P.op("pool", lambda e: e.iota(ji[:], pattern=[[1, 513]], base=0, channel_multiplier=0), [], [ji])
        P.op("dve", lambda e: e.tensor_copy(out=jf[:], in_=ji[:]), [ji], [jf])
        w = {n: self.A("p_w" + n, [16], F32) for n in ("dt", "ard", "sn", "cs", "lbr", "lbi", "den", "m1", "t1", "t2",
                                                        "kr", "ki", "nki", "f1", "f2")}
        wi = self.A("p_wi", [16], I32)
        tb = self.A("p_tb", [16], F32)

        def dve(fn, reads, writes):
            P.op("dve", fn, reads, writes)

        def act(fn, reads, writes):
            P.op("act", fn, reads, writes)
        act(lambda e: e.activation(out=w["dt"][:], in_=t["ldt"][:], func=AF.Exp), [t["ldt"]], [w["dt"]])
        dve(lambda e: e.tensor_tensor(out=w["ard"][:], in0=t["ar"][:], in1=w["dt"][:], op=ALU.mult), [t["ar"], w["dt"]], [w["ard"]])
        dve(lambda e: e.tensor_tensor(out=th[:], in0=t["ai"][:], in1=w["dt"][:], op=ALU.mult), [t["ai"], w["dt"]], [th])
        dve(lambda e: e.tensor_scalar(out=th[:], in0=th[:], scalar1=1.0 / TWO_PI, scalar2=None, op0=ALU.mult), [th], [th])
        act(lambda e: e.activation(out=rr[:], in_=w["ard"][:], func=AF.Exp), [w["ard"]], [rr])
        self.dbgt = self.sb("p_dbgt", [128, 4, 16], F32)
        dbgt = self.dbgt
        dve(lambda e: e.tensor_copy(out=dbgt[:, 0, :], in_=th[:]), [th], [dbgt])
        for (dst, shift) in (("sn", 0.0), ("cs", 0.25)):
            dve(lambda e, shift=shift: e.tensor_scalar(out=w["f1"][:], in0=th[:], scalar1=shift, scalar2=None, op0=ALU.add),
                [th], [w["f1"]])
            dve(lambda e: e.tensor_copy(out=wi[:], in_=w["f1"][:]), [w["f1"]], [wi])
            dve(lambda e: e.tensor_copy(out=w["f2"][:], in_=wi[:]), [wi], [w["f2"]])
            dve(lambda e: e.tensor_tensor(out=w["f1"][:], in0=w["f1"][:], in1=w["f2"][:], op=ALU.subtract),
                [w["f1"], w["f2"]], [w["f1"]])
            act(lambda e, dst=dst: e.activation(out=w[dst][:], in_=w["f1"][:], func=AF.Sin, scale=TWO_PI),
                [w["f1"]], [w[dst]])
        dve(lambda e: e.tensor_copy(out=dbgt[:, 1, :], in_=th[:]), [th], [dbgt])
        dve(lambda e: e.tensor_copy(out=dbgt[:, 2, :], in_=w["sn"][:]), [w["sn"]], [dbgt])
        dve(lambda e: e.tensor_copy(out=dbgt[:, 3, :], in_=w["cs"][:]), [w["cs"]], [dbgt])
        dve(lambda e: e.tensor_tensor(out=w["lbr"][:], in0=rr[:], in1=w["cs"][:], op=ALU.mult), [rr, w["cs"]], [w["lbr"]])
        dve(lambda e: e.tensor_tensor(out=w["lbi"][:], in0=rr[:], in1=w["sn"][:], op=ALU.mult), [rr, w["sn"]], [w["lbi"]])
        dve(lambda e: e.tensor_tensor(out=w["den"][:], in0=t["ar"][:], in1=t["ar"][:], op=ALU.mult), [t["ar"]], [w["den"]])
        dve(lambda e: e.tensor_tensor(out=w["t1"][:], in0=t["ai"][:], in1=t["ai"][:], op=ALU.mult), [t["ai"]], [w["t1"]])
        dve(lambda e: e.tensor_tensor(out=w["den"][:], in0=w["den"][:], in1=w["t1"][:], op=ALU.add), [w["den"], w["t1"]], [w["den"]])
        dve(lambda e: e.reciprocal(out=w["den"][:], in_=w["den"][:]), [w["den"]], [w["den"]])
        dve(lambda e: e.tensor_scalar(out=w["m1"][:], in0=w["lbr"][:], scalar1=-1.0, scalar2=None, op0=ALU.add), [w["lbr"]], [w["m1"]])
        dve(lambda e: e.tensor_tensor(out=w["t1"][:], in0=w["m1"][:], in1=t["ar"][:], op=ALU.mult), [w["m1"], t["ar"]], [w["t1"]])
        dve(lambda e: e.tensor_tensor(out=w["t2"][:], in0=w["lbi"][:], in1=t["ai"][:], op=ALU.mult), [w["lbi"], t["ai"]], [w["t2"]])
        dve(lambda e: e.tensor_tensor(out=w["t1"][:], in0=w["t1"][:], in1=w["t2"][:], op=ALU.add), [w["t1"], w["t2"]], [w["t1"]])
        dve(lambda e: e.tensor_tensor(out=w["kr"][:], in0=w["t1"][:], in1=w["den"][:], op=ALU.mult), [w["t1"], w["den"]], [w["kr"]])
        dve(lambda e: e.tensor_tensor(out=w["t1"][:], in0=w["lbi"][:], in1=t["ar"][:], op=ALU.mult), [w["lbi"], t["ar"]], [w["t1"]])
        dve(lambda e: e.tensor_tensor(out=w["t2"][:], in0=w["m1"][:], in1=t["ai"][:], op=ALU.mult), [w["m1"], t["ai"]], [w["t2"]])
        dve(lambda e: e.tensor_tensor(out=w["t1"][:], in0=w["t1"][:], in1=w["t2"][:], op=ALU.subtract), [w["t1"], w["t2"]], [w["t1"]])
        dve(lambda e: e.tensor_tensor(out=w["ki"][:], in0=w["t1"][:], in1=w["den"][:], op=ALU.mult), [w["t1"], w["den"]], [w["ki"]])
        dve(lambda e: e.tensor_scalar(out=w["nki"][:], in0=w["ki"][:], scalar1=-1.0, scalar2=None, op0=ALU.mult), [w["ki"]], [w["nki"]])
        for dg in range(16):
            gp = dg % 8
            dve(lambda e, dg=dg, gp=gp: e.tensor_scalar(out=tb[:], in0=t["br"][:, gp, :], scalar1=w["kr"][:, dg:dg + 1],
                                                        scalar2=None, op0=ALU.mult), [t["br"], w["kr"]], [tb])
            dve(lambda e, dg=dg, gp=gp: e.scalar_tensor_tensor(out=bbr[:, dg, :], in0=t["bi"][:, gp, :],
                                                               scalar=w["nki"][:, dg:dg + 1], in1=tb[:],
                                                               op0=ALU.mult, op1=ALU.add), [t["bi"], w["nki"], tb], [bbr])
            dve(lambda e, dg=dg, gp=gp: e.tensor_scalar(out=tb[:], in0=t["bi"][:, gp, :], scalar1=w["kr"][:, dg:dg + 1],
                                                        scalar2=None, op0=ALU.mult), [t["bi"], w["kr"]], [tb])
            dve(lambda e, dg=dg, gp=gp: e.scalar_tensor_tensor(out=bbi[:, dg, :], in0=t["br"][:, gp, :],
                                                               scalar=w["ki"][:, dg:dg + 1], in1=tb[:],
                                                               op0=ALU.mult, op1=ALU.add), [t["br"], w["ki"], tb], [bbi])
        return dict(rr=rr, th=th, bbr=bbr, bbi=bbi, cr=t["cr"], ci=t["ci"], d=t["d"], jf=jf)

    def phase_c(self, sp_in, UT, G_loc):
        P = self.P
        ident = self.ident
        prm = self.ssm_params(sp_in)
        self.new_phase()
        rr, th, bbr, bbi, jf = prm["rr"], prm["th"], prm["bbr"], prm["bbi"], prm["jf"]
        NT = 513
        dbgc = "Cdbg" in set((self.debug or "").split(","))
        dump = self.dump
        if dbgc:
            dump(self.dbgt[:].rearrange("p a b -> p (a b)"), 64, [self.dbgt])
            dump(rr[:], 16, [rr])
            dump(th[:], 16, [th])
            dump(bbr[:].rearrange("p a b -> p (a b)"), 256, [bbr])
            dump(bbi[:].rearrange("p a b -> p (a b)"), 256, [bbi])
            dump(jf[:], 513, [jf])
        tc = self.A("c_tc", [8, NT], F32)
        ts = self.A("c_ts", [8, NT], F32)
        Bp = self.A("c_Bp", [8, 2, 128], BF16)
        Cp = self.A("c_Cp", [8, 2, 128], BF16)
        BT = self.A("c_BT", [2, 128], BF16)
        Useq = self.A("c_U", [8 * NP_ + NMETA], BF16)
        Y = self.A("c_Y", [8 * NP_ + NMETA], F32)
        tj = self.A("c_tj", [NT], F32)
        tj2 = self.A("c_tj2", [NT], F32)
        tji = self.A("c_tji", [NT], I32)
        cinit = self.A("c_ci", [4, 2], F32)
        ctmp = self.A("c_ct", [4, 2], F32)
        sets = []
        for i in range(2):
            sets.append(dict(
                Bri=self.A(f"c_Bri{i}", [2, 512], F32), T1=self.A(f"c_T1{i}", [512], F32),
                T2=self.A(f"c_T2{i}", [512], F32), T3=self.A(f"c_T3{i}", [512], F32),
                T4=self.A(f"c_T4{i}", [512], F32), Wr=self.A(f"c_Wr{i}", [512], F32),
                Wi=self.A(f"c_Wi{i}", [512], F32), Xr=self.A(f"c_Xr{i}", [512], BF16),
                Xi=self.A(f"c_Xi{i}", [512], BF16)))
        gst = [self.A(f"c_g{i}", [512], BF16) for i in range(2)]
        BU = [Res(f"c_bu{i}", self.PS2[i]) for i in range(2)]
        Yps = [self.bank(4, "c_y0"), self.bank(6, "c_y1")]
        psB = self.bank(5, "c_psB", BF16, [8, 128])
        ui = 0
        yi = 0
        gi = 0
        for ct in range(2):
            for dgl in range(8):
                d_, gpl = dgl // 4, dgl % 4
                dg = d_ * 8 + ct * 4 + gpl
                P.op("dve", lambda e, dg=dg: e.tensor_scalar(out=tj[:], in0=jf[:], scalar1=th[:, dg:dg + 1], scalar2=None,
                                                             op0=ALU.mult), [jf, th], [tj])
                for (dst, shift) in ((ts, 0.0), (tc, 0.25)):
                    P.op("dve", lambda e, shift=shift: e.tensor_scalar(out=tj2[:], in0=tj[:], scalar1=shift, scalar2=None,
                                                                       op0=ALU.add), [tj], [tj2])
                    P.op("dve", lambda e: e.tensor_copy(out=tji[:], in_=tj2[:]), [tj2], [tji])
                    self._frac_sin(dst, dgl, tj2, tji)
                for ri, src in enumerate((bbr, bbi)):
                    P.op("dve", lambda e, ri=ri: e.memset(BT[:, ri, :], 0.0), [], [BT])
                    for g2 in range(2):
                        band = (2 * gpl + g2) * 16
                        P.op("dve", lambda e, ri=ri, src=src, g2=g2, band=band, dg=dg: e.tensor_copy(
                            out=BT[g2 * 64:(g2 + 1) * 64, ri, band:band + 16], in_=src[g2 * 64:(g2 + 1) * 64, dg, :]),
                            [src], [BT])
                    P.op("pe", lambda e, ri=ri: e.transpose(out=psB[:, ri, :], in_=BT[:, ri, :], identity=ident[:]),
                         [BT, ident], [psB])
                P.op("act", lambda e, dgl=dgl: e.activation(out=Bp[:, dgl, :, :], in_=psB[:, 0:2, :], func=AF.Copy),
                     [psB], [Bp])
                for ri, (src, sgn) in enumerate(((prm["cr"], 1.0), (prm["ci"], -1.0))):
                    P.op("dve", lambda e, dgl=dgl, ri=ri: e.memset(Cp[:, dgl, ri, :], 0.0), [], [Cp])
                    for g2 in range(2):
                        band = (2 * gpl + g2) * 16
                        P.op("dve", lambda e, dgl=dgl, ri=ri, src=src, sgn=sgn, g2=g2, band=band, dg=dg: e.tensor_scalar(
                            out=Cp[g2 * 64:(g2 + 1) * 64, dgl, ri, band:band + 16], in0=src[g2 * 64:(g2 + 1) * 64, dg, :],
                            scalar1=sgn, scalar2=None, op0=ALU.mult), [src], [Cp])
            if dbgc and ct == 0:
                dump(tc[:].rearrange("p a b -> p (a b)"), 8 * NT, [tc])
                dump(ts[:].rearrange("p a b -> p (a b)"), 8 * NT, [ts])
                self.dbg_stage = Res("c_dbgst", Y[:, 0:4096])
                P.op("dve", lambda e: e.tensor_copy(out=self.dbg_stage[:, 0:2048], in_=Bp[:].rearrange("p a b c -> p (a b c)")), [Bp], [Y])
                P.op("dve", lambda e: e.tensor_copy(out=self.dbg_stage[:, 2048:4096], in_=Cp[:].rearrange("p a b c -> p (a b c)")), [Cp], [Y])
                dump(Y[:, 0:4096], 4096, [Y])
                return
            for (slot0, nper) in self.seqs():
                L = 8 * nper + NMETA
                P.dma("sp", lambda e, ct=ct: e.dma_start(out=Useq[:, 0:NMETA], in_=UT[ct * 128:(ct + 1) * 128, TL:TL + NMETA]),
                      [UT], [], wdisj=[Useq])
                for r in range(NCORE):
                    P.dma("sp" if r % 2 == 0 else "act", lambda e, ct=ct, r=r, slot0=slot0, nper=nper: e.dma_start(
                        out=Useq[:, NMETA + r * nper:NMETA + (r + 1) * nper],
                        in_=UT[ct * 128:(ct + 1) * 128, r * TLP + slot0:r * TLP + slot0 + nper]), [UT], [], wdisj=[Useq])
                chunks_f = [(0, NMETA)] + [(NMETA + i * 512, 512) for i in range(8 * nper // 512)]
                chunks_b = list(reversed(chunks_f[1:]))
                for d_, chunks in ((0, chunks_f), (1, chunks_b)):
                    for ci_, (c0, n) in enumerate(chunks):
                        yp = Yps[yi % 2]
                        yi += 1
                        for gpl in range(4):
                            dgl = d_ * 4 + gpl
                            dg = d_ * 8 + ct * 4 + gpl
                            S_ = sets[ui % 2]
                            bu = BU[ui % 2]
                            ui += 1
                            Bri, T1, T2, T3, T4, Wr, Wi, Xr, Xi = (S_[k] for k in ("Bri", "T1", "T2", "T3", "T4", "Wr", "Wi", "Xr", "Xi"))
                            for ri in range(2):
                                P.op("pe", lambda e, bu=bu, ri=ri, dgl=dgl, c0=c0, n=n: e.matmul(
                                    bu[:, ri, 0:n], lhsT=Bp[:, dgl, ri, :], rhs=Useq[:, c0:c0 + n], start=True, stop=True),
                                    [Bp, Useq], [bu])
                            P.op("act", lambda e, Bri=Bri, bu=bu, n=n: e.activation(out=Bri[:, :, 0:n], in_=bu[:, :, 0:n], func=AF.Copy),
                                 [bu], [Bri])
                            if d_ == 0:
                                cv = tc[:, dgl, 0:n]
                                sv = ts[:, dgl, 0:n]
                                rev = lambda ap: ap
                            else:
                                cv = tc[:, dgl, 0:n][:, ::-1]
                                sv = ts[:, dgl, 0:n][:, ::-1]
                                rev = lambda ap: ap[:, ::-1]
                            P.op("dve", lambda e, T1=T1, Bri=Bri, cv=cv, n=n: e.tensor_tensor(out=T1[:, 0:n], in0=Bri[:, 0, 0:n], in1=cv, op=ALU.mult), [Bri, tc], [T1])
                            P.op("dve", lambda e, T2=T2, Bri=Bri, sv=sv, n=n: e.tensor_tensor(out=T2[:, 0:n], in0=Bri[:, 1, 0:n], in1=sv, op=ALU.mult), [Bri, ts], [T2])
                            P.op("dve", lambda e, T1=T1, T2=T2, n=n: e.tensor_tensor(out=T1[:, 0:n], in0=T1[:, 0:n], in1=T2[:, 0:n], op=ALU.add), [T1, T2], [T1])
                            P.op("pool", lambda e, T3=T3, Bri=Bri, cv=cv, n=n: e.tensor_tensor(out=T3[:, 0:n], in0=Bri[:, 1, 0:n], in1=cv, op=ALU.mult), [Bri, tc], [T3])
                            P.op("pool", lambda e, T4=T4, Bri=Bri, sv=sv, n=n: e.tensor_tensor(out=T4[:, 0:n], in0=Bri[:, 0, 0:n], in1=sv, op=ALU.mult), [Bri, ts], [T4])
                            P.op("pool", lambda e, T3=T3, T4=T4, n=n: e.tensor_tensor(out=T3[:, 0:n], in0=T3[:, 0:n], in1=T4[:, 0:n], op=ALU.subtract), [T3, T4], [T3])
                            rb = rr[:, dg:dg + 1].to_broadcast([128, n])
                            first = (ci_ == 0)
                            for (Wt, Tin, col) in ((Wr, T1, 0), (Wi, T3, 1)):
                                init = 0.0 if first else cinit[:, gpl, col:col + 1]
                                P.op("dve", lambda e, Wt=Wt, Tin=Tin, init=init, rb=rb, n=n, rev=rev: e.tensor_tensor_scan(
                                    out=rev(Wt[:, 0:n]), data0=rb, data1=rev(Tin[:, 0:n]), initial=init,
                                    op0=ALU.mult, op1=ALU.add), [Tin, rr, cinit], [Wt])
                            lastc = (n - 1) if d_ == 0 else 0
                            cn = tc[:, dgl, n:n + 1]
                            sn = ts[:, dgl, n:n + 1]
                            P.op("dve", lambda e, gpl=gpl, Wi=Wi, sn=sn, lastc=lastc: e.tensor_tensor(
                                out=ctmp[:, gpl, 0:1], in0=Wi[:, lastc:lastc + 1], in1=sn, op=ALU.mult), [Wi, ts], [ctmp])
                            P.op("dve", lambda e, gpl=gpl, Wr=Wr, cn=cn, lastc=lastc: e.scalar_tensor_tensor(
                                out=cinit[:, gpl, 0:1], in0=Wr[:, lastc:lastc + 1], scalar=cn, in1=ctmp[:, gpl, 0:1],
                                op0=ALU.mult, op1=ALU.subtract), [Wr, tc, ctmp], [cinit])
                            P.op("dve", lambda e, gpl=gpl, Wi=Wi, cn=cn, lastc=lastc: e.tensor_tensor(
                                out=ctmp[:, gpl, 1:2], in0=Wi[:, lastc:lastc + 1], in1=cn, op=ALU.mult), [Wi, tc], [ctmp])
                            P.op("dve", lambda e, gpl=gpl, Wr=Wr, sn=sn, lastc=lastc: e.scalar_tensor_tensor(
                                out=cinit[:, gpl, 1:2], in0=Wr[:, lastc:lastc + 1], scalar=sn, in1=ctmp[:, gpl, 1:2],
                                op0=ALU.mult, op1=ALU.add), [Wr, ts, ctmp], [cinit])
                            P.op("dve", lambda e, T1=T1, Wr=Wr, cv=cv, n=n: e.tensor_tensor(out=T1[:, 0:n], in0=Wr[:, 0:n], in1=cv, op=ALU.mult), [Wr, tc], [T1])
                            P.op("dve", lambda e, T2=T2, Wi=Wi, sv=sv, n=n: e.tensor_tensor(out=T2[:, 0:n], in0=Wi[:, 0:n], in1=sv, op=ALU.mult), [Wi, ts], [T2])
                            P.op("dve", lambda e, Xr=Xr, T1=T1, T2=T2, n=n: e.tensor_tensor(out=Xr[:, 0:n], in0=T1[:, 0:n], in1=T2[:, 0:n], op=ALU.subtract), [T1, T2], [Xr])
                            P.op("pool", lambda e, T3=T3, Wr=Wr, sv=sv, n=n: e.tensor_tensor(out=T3[:, 0:n], in0=Wr[:, 0:n], in1=sv, op=ALU.mult), [Wr, ts], [T3])
                            P.op("pool", lambda e, T4=T4, Wi=Wi, cv=cv, n=n: e.tensor_tensor(out=T4[:, 0:n], in0=Wi[:, 0:n], in1=cv, op=ALU.mult), [Wi, tc], [T4])
                            P.op("pool", lambda e, Xi=Xi, T3=T3, T4=T4, n=n: e.tensor_tensor(out=Xi[:, 0:n], in0=T3[:, 0:n], in1=T4[:, 0:n], op=ALU.add), [T3, T4], [Xi])
                            P.op("pe", lambda e, yp=yp, dgl=dgl, Xr=Xr, n=n, gpl=gpl: e.matmul(
                                yp[:, 0:n], lhsT=Cp[:, dgl, 0, :], rhs=Xr[:, 0:n], start=(gpl == 0), stop=False), [Cp, Xr], [yp])
                            P.op("pe", lambda e, yp=yp, dgl=dgl, Xi=Xi, n=n, gpl=gpl: e.matmul(
                                yp[:, 0:n], lhsT=Cp[:, dgl, 1, :], rhs=Xi[:, 0:n], start=False, stop=(gpl == 3)), [Cp, Xi], [yp])
                        if d_ == 0:
                            P.op("act", lambda e, yp=yp, c0=c0, n=n: e.activation(out=Y[:, c0:c0 + n], in_=yp[:, 0:n], func=AF.Copy),
                                 [yp], [], wdisj=[Y])
                        else:
                            P.op("dve", lambda e, yp=yp, c0=c0, n=n: e.tensor_tensor(out=Y[:, c0:c0 + n], in0=Y[:, c0:c0 + n],
                                                                                     in1=yp[:, 0:n], op=ALU.add), [yp, Y], [], wdisj=[Y])
                S_ = sets[0]
                T1, T2 = S_["T1"], S_["T2"]
                for r in range(NCORE):
                    for pc in range(nper // 512):
                        c0 = NMETA + r * nper + pc * 512
                        g = gst[gi % 2]
                        gi += 1
                        yv = Y[:, c0:c0 + 512]
                        P.op("dve", lambda e, yv=yv, c0=c0, ct=ct: e.scalar_tensor_tensor(
                            out=yv, in0=Useq[:, c0:c0 + 512], scalar=prm["d"][:, ct:ct + 1], in1=yv, op0=ALU.mult, op1=ALU.add),
                            [Useq, prm["d"], Y], [], wdisj=[Y])
                        P.op("pool", lambda e, yv=yv, T1=T1: e.tensor_tensor(out=T1[:], in0=yv, in1=yv, op=ALU.mult), [Y], [T1])
                        P.op("dve", lambda e, T1=T1: e.tensor_scalar(out=T1[:], in0=T1[:], scalar1=0.044715, scalar2=1.0,
                                                                     op0=ALU.mult, op1=ALU.add), [T1], [T1])
                        P.op("pool", lambda e, yv=yv, T1=T1: e.tensor_tensor(out=T1[:], in0=T1[:], in1=yv, op=ALU.mult), [Y, T1], [T1])
                        P.op("act", lambda e, T1=T1, T2=T2: e.activation(out=T2[:], in_=T1[:], func=AF.Sigmoid, scale=1.5957691216057308),
                             [T1], [T2])
                        P.op("dve", lambda e, g=g, yv=yv, T2=T2: e.tensor_tensor(out=g[:], in0=yv, in1=T2[:], op=ALU.mult), [Y, T2], [g])
                        P.dma("sp", lambda e, g=g, ct=ct, r=r, slot0=slot0, pc=pc: e.dma_start(
                            out=G_loc[ct * 128:(ct + 1) * 128, r * GS + slot0 + pc * 512:r * GS + slot0 + (pc + 1) * 512], in_=g[:]),
                            [g], [], wdisj=[G_loc])

    def phase_e(self, x_loc, w_glu, bgT, w_out, wnT, fnw, idxG, WGB, WOB, G_all, AG_loc, ZT_loc, out_loc):
        P = self.P
        ones = self.ones
        for r0 in range(0, 2048, 512):
            P.dma("pool", lambda e, r0=r0: e.dma_start(out=WGB[r0:r0 + 512, :], in_=w_glu[r0:r0 + 512, :]),
                  [w_glu], [], wdisj=[WGB])
        wn = self.A("e_wn", [32], F32)
        bg = self.A("e_bg", [16], F32)
        fw = self.A("e_fw", [D], F32)
        idx = self.A("e_idx", [16, 8], I32)
        P.dma("sp", lambda e: e.dma_start(out=wn[:], in_=wnT[:, :]), [wnT], [wn])
        P.dma("sp", lambda e: e.dma_start(out=bg[:], in_=bgT[:, :]), [bgT], [bg])
        P.dma("sp", lambda e: e.dma_start(out=fw[:], in_=fnw[0:1, :].partition_broadcast(128)), [fnw], [fw])
        P.dma("sp", lambda e: e.dma_start(out=idx[:], in_=idxG[:, :, :]), [idxG], [idx])
        Y = self.A("e_y", [4, D], F32)
        mixT = self.A("e_mix", [32, 512], BF16)
        Wc = [self.A(f"e_W{i}", [32, 256], BF16) for i in range(2)]
        gT = self.A("e_gT", [16, 512], BF16)
        junk = self.A("e_junk", [D], BF16)
        zs = [self.A(f"e_zs{i}", [512], BF16) for i in range(2)]
        sg = [self.A(f"e_sg{i}", [512], F32) for i in range(2)]
        sq = [self.A(f"e_sq{i}", [512], BF16) for i in range(3)]
        st = self.A("e_st", [16], F32)
        rs = self.A("e_rs", [16], F32)
        for k in range(32):
            wf = Y[:, k % 4, :]
            wb = junk
            P.dma("sp" if k % 2 == 0 else "act", lambda e, wf=wf, k=k: e.dma_start(out=wf, in_=w_out[k * 128:(k + 1) * 128, :]),
                  [w_out], [Y])
            P.op("dve" if k % 2 == 0 else "pool", lambda e, wf=wf, k=k: e.tensor_scalar(
                out=junk[:], in0=wf, scalar1=wn[:, k:k + 1], scalar2=None, op0=ALU.mult), [Y, wn], [junk])
            P.dma("sp", lambda e, k=k: e.dma_start(out=WOB[k * 128:(k + 1) * 128, :], in_=junk[:]), [junk], [], wdisj=[WOB])
        psG = [self.bank(i, f"e_pg{i}") for i in range(2)]
        psA = [self.bank(2 + i, f"e_pa{i}") for i in range(4)]
        psN = [self.bank(6, "e_pn0"), self.bank(7, "e_pn1")]
        rrep = [self.A(f"e_rrep{i}", [512], F32) for i in range(2)]
        G2 = G_all[:, :].rearrange("r (h c) -> (r h) c", c=512)
        wi = 0
        gi = 0
        zi = 0
        qi = 0
        oi = 0
        for blk in range(8):
            tok0 = blk * 512
            P.dma("sp", lambda e, tok0=tok0: e.dma_start(
                out=mixT[:, 0:16, :], in_=AG_loc[:, tok0:tok0 + 512].rearrange("(k p) t -> p k t", p=128)),
                [AG_loc], [mixT])
            for k in range(16):
                P.dma("pool", lambda e, k=k, blk=blk: e.indirect_dma_start(
                    out=gT[:, k, :], out_offset=None, in_=G2,
                    in_offset=bass.IndirectOffsetOnAxis(ap=idx[:, k, blk:blk + 1], axis=0)), [G_all, idx], [], wdisj=[gT])
            for i in range(4):
                P.dma("act", lambda e, i=i, tok0=tok0: e.dma_start(out=Y[:, i, :], in_=x_loc[tok0 + i * 128:tok0 + (i + 1) * 128, :]),
                      [x_loc], [], wdisj=[Y])
            for cc in range(8):
                W = Wc[wi % 2]
                wi += 1
                P.dma("sp", lambda e, W=W, cc=cc: e.dma_start(
                    out=W[:, 0:16, :], in_=WGB[:, cc * 256:(cc + 1) * 256].rearrange("(k p) c -> p k c", p=128)), [WGB], [W])
                for jj in range(2):
                    j = cc * 2 + jj
                    pg = psG[gi % 2]
                    sgt = sg[gi % 2]
                    gi += 1
                    z = zs[zi % 2]
                    zi += 1
                    P.dma("act", lambda e, z=z, j=j, tok0=tok0: e.dma_start(
                        out=z[:], in_=ZT_loc[2048 + j * 128:2048 + (j + 1) * 128, tok0:tok0 + 512]), [ZT_loc], [z])
                    for k in range(16):
                        P.op("pe", lambda e, pg=pg, W=W, k=k, jj=jj: e.matmul(
                            pg[:], lhsT=W[:, k, jj * 128:(jj + 1) * 128], rhs=gT[:, k, :], start=(k == 0), stop=(k == 15)),
                            [W, gT], [pg])
                    P.op("act", lambda e, sgt=sgt, pg=pg, j=j: e.activation(out=sgt[:], in_=pg[:], func=AF.Sigmoid,
                                                                          bias=bg[:, j:j + 1]), [pg, bg], [sgt])
                    P.op("dve", lambda e, sgt=sgt, j=j: e.tensor_tensor(out=sgt[:], in0=sgt[:], in1=gT[:, j, :], op=ALU.mult),
                         [sgt, gT], [sgt])
                    P.op("pool", lambda e, sgt=sgt, z=z, j=j: e.tensor_tensor(out=mixT[:, 16 + j, :], in0=sgt[:], in1=z[:], op=ALU.mult),
                         [sgt, z], [], wdisj=[mixT])
            for half in range(2):
                pn = psN[half]
                for k in range(16):
                    kk = half * 16 + k
                    sqt = sq[qi % 3]
                    qi += 1
                    P.op("pool" if kk % 2 == 0 else "dve", lambda e, sqt=sqt, kk=kk: e.tensor_tensor(
                        out=sqt[:], in0=mixT[:, kk, :], in1=mixT[:, kk, :], op=ALU.mult), [mixT], [sqt])
                    P.op("pe", lambda e, sqt=sqt, pn=pn, k=k: e.matmul(
                        pn[:], lhsT=ones[:, :], rhs=sqt[:], start=(k == 0), stop=(k == 15)), [sqt, ones], [pn])
                rr_ = rrep[half]
                P.op("dve", lambda e, rr_=rr_, pn=pn: e.tensor_scalar(out=rr_[:], in0=pn[:], scalar1=1.0 / 2048, scalar2=EPS,
                                                                    op0=ALU.mult, op1=ALU.add), [pn], [rr_])
                P.op("act", lambda e, rr_=rr_: e.activation(out=rr_[:], in_=rr_[:], func=AF.Sqrt), [rr_], [rr_])
                P.op("dve", lambda e, rr_=rr_: e.reciprocal(out=rr_[:], in_=rr_[:]), [rr_], [rr_])
            for half in range(2):
                rr_ = rrep[half]
                for k in range(16):
                    kk = half * 16 + k
                    P.op("pool" if kk % 2 == 0 else "dve", lambda e, rr_=rr_, kk=kk: e.tensor_tensor(
                        out=mixT[:, kk, :], in0=mixT[:, kk, :], in1=rr_[:], op=ALU.mult), [mixT, rr_], [mixT])
            for cc in range(16):
                W = Wc[wi % 2]
                wi += 1
                P.dma("sp", lambda e, W=W, cc=cc: e.dma_start(
                    out=W[:], in_=WOB[:, cc * 256:(cc + 1) * 256].rearrange("(k p) c -> p k c", p=128)), [WOB], [W])
                for i in range(4):
                    pa = psA[oi % 4]
                    oi += 1
                    for k in range(32):
                        P.op("pe", lambda e, pa=pa, W=W, k=k, i=i: e.matmul(
                            pa[:, 0:256], lhsT=mixT[:, k, i * 128:(i + 1) * 128], rhs=W[:, k, :], start=(k == 0), stop=(k == 31)),
                            [mixT, W], [pa])
                    yv = Y[:, i, cc * 256:(cc + 1) * 256]
                    P.op("dve", lambda e, yv=yv, pa=pa: e.tensor_tensor(out=yv, in0=pa[:, 0:256], in1=yv, op=ALU.add),
                         [pa, Y], [], wdisj=[Y])
            for i in range(4):
                P.op("dve", lambda e, i=i: e.memset(st[:, i:i + 1], 0.0), [], [st])
                P.op("act", lambda e, i=i: e.activation(out=junk[:], in_=Y[:, i, :], func=AF.Square, accum_out=st[:, i:i + 1]),
                     [Y, st], [junk, st])
                P.op("dve", lambda e, i=i: e.tensor_scalar(out=rs[:, 8 + i:9 + i], in0=st[:, i:i + 1], scalar1=1.0 / D, scalar2=EPS,
                                                           op0=ALU.mult, op1=ALU.add), [st], [rs])
                P.op("act", lambda e, i=i: e.activation(out=rs[:, 8 + i:9 + i], in_=rs[:, 8 + i:9 + i], func=AF.Sqrt), [rs], [rs])
                P.op("dve", lambda e, i=i: e.reciprocal(out=rs[:, 8 + i:9 + i], in_=rs[:, 8 + i:9 + i]), [rs], [rs])
                P.op("dve", lambda e, i=i: e.scalar_tensor_tensor(
                    out=Y[:, i, :], in0=Y[:, i, :], scalar=rs[:, 8 + i:9 + i], in1=fw[:], op0=ALU.mult, op1=ALU.mult),
                    [Y, rs, fw], [Y])
                P.dma("sp", lambda e, i=i, tok0=tok0: e.dma_start(out=out_loc[tok0 + i * 128:tok0 + (i + 1) * 128, :], in_=Y[:, i, :]),
                      [Y], [], wdisj=[out_loc])

    def _frac_sin(self, dst, dgl, tj2, tji):
        P = self.P
        if not hasattr(self, "_tjf"):
            self._tjf = self.A("c_tjf", [513], F32)
        tjf = self._tjf
        P.op("dve", lambda e: e.tensor_copy(out=tjf[:], in_=tji[:]), [tji], [tjf])
        P.op("dve", lambda e: e.tensor_tensor(out=tjf[:], in0=tj2[:], in1=tjf[:], op=ALU.subtract), [tj2, tjf], [tjf])
        P.op("act", lambda e, dst=dst, dgl=dgl: e.activation(out=dst[:, dgl, :], in_=tjf[:], func=AF.Sin, scale=TWO_PI),
             [tjf], [dst])


def _rope_tables():
    inv = (10000.0 ** (-np.arange(32, dtype=np.float32) / 32)).astype(np.float32)

    def tab(s_idx):
        rows = (s_idx // 64).astype(np.float32)
        cols = (s_idx % 64).astype(np.float32)
        ang = np.concatenate([rows[:, None] * inv, cols[:, None] * inv], axis=1).astype(np.float32)
        return np.concatenate([np.cos(ang), np.sin(ang)], axis=1).astype(np.float32)
    return tab


def make_in_maps(inp):
    tab = _rope_tables()
    f = lambda k: np.asarray(inp[k], np.float32)
    xp = f("x_prompt")[0]
    xs = f("x_sample")
    meta = f("meta_tokens")
    w_in = f("w_in")[0]
    w_main = np.ascontiguousarray(np.concatenate([w_in[:, :5120], w_in[:, 7168:]], axis=1))
    nwT = np.ascontiguousarray(f("norm_w")[0].reshape(32, 128).T)
    qkw = np.stack([f("q_norm_w")[0], f("k_norm_w")[0]])
    ident = np.eye(128, dtype=np.float32).astype(ml_dtypes.bfloat16)
    w_glu = np.ascontiguousarray(f("w_glu")[0])
    w_out = np.ascontiguousarray(f("w_out")[0])
    bgT = np.ascontiguousarray(f("b_glu")[0].reshape(16, 128).T)
    wnT = np.ascontiguousarray(np.concatenate([f("attn_out_norm_w")[0], f("ssm_out_norm_w")[0]]).reshape(32, 128).T)
    fnw = f("final_norm_w").reshape(1, D)
    a_re, a_im, ldt = f("ssm_a_re")[0], f("ssm_a_im")[0], f("ssm_log_dt")[0]
    b_re, b_im = f("ssm_b_re")[0], f("ssm_b_im")[0]
    c_re, c_im = f("ssm_c_re")[0], f("ssm_c_im")[0]
    dvec = f("ssm_d")[0]
    maps = []
    for c in range(NCORE):
        x_loc = np.zeros((TLP, D), np.float32)
        x_loc[0:NP_] = xp[c * NP_:(c + 1) * NP_]
        for j in range(4):
            x_loc[NP_ + j * NS_:NP_ + (j + 1) * NS_] = xs[j, c * NS_:(c + 1) * NS_]
        x_loc[TL:TL + NMETA] = meta
        cs = np.zeros((TLP, 128), np.float32)
        cs[:, 0:64] = 1.0
        cs[0:NP_] = tab(np.arange(c * NP_, (c + 1) * NP_))
        for j in range(4):
            cs[NP_ + j * NS_:NP_ + (j + 1) * NS_] = tab(np.arange(c * NS_, (c + 1) * NS_))
        w_u = np.ascontiguousarray(w_in[:, 5120 + c * 256:5120 + (c + 1) * 256])
        g0 = 16 * c
        s_ar = np.zeros((128, 16), np.float32)
        s_ai = np.zeros((128, 16), np.float32)
        s_ldt = np.zeros((128, 16), np.float32)
        s_br = np.zeros((128, 8, 16), np.float32)
        s_bi = np.zeros((128, 8, 16), np.float32)
        s_cr = np.zeros((128, 16, 16), np.float32)
        s_ci = np.zeros((128, 16, 16), np.float32)
        for gp in range(8):
            for g2 in range(2):
                g = g0 + 2 * gp + g2
                sl = slice(g2 * 64, (g2 + 1) * 64)
                s_br[sl, gp, :] = b_re[g]
                s_bi[sl, gp, :] = b_im[g]
                for d_ in range(2):
                    s_ar[sl, d_ * 8 + gp] = a_re[d_, g]
                    s_ai[sl, d_ * 8 + gp] = a_im[d_, g]
                    s_ldt[sl, d_ * 8 + gp] = ldt[d_, g]
                    s_cr[sl, d_ * 8 + gp, :] = c_re[d_, g].T
                    s_ci[sl, d_ * 8 + gp, :] = c_im[d_, g].T
        s_d = np.ascontiguousarray(dvec[256 * c:256 * (c + 1)].reshape(2, 128).T)
        idxG = np.zeros((128, 16, 8), np.int32)
        for k in range(16):
            for blk in range(8):
                idxG[:, k, blk] = (128 * k + np.arange(128)) * (NCORE * GS // 512) + c * (GS // 512) + blk
        maps.append({"x_loc": x_loc, "cs_loc": cs, "nwT": nwT, "w_main": w_main, "w_u": w_u,
                     "qkw": qkw, "ident_in": ident, "s_ar": s_ar, "s_ai": s_ai, "s_ldt": s_ldt, "s_br": s_br,
                     "s_bi": s_bi, "s_cr": s_cr, "s_ci": s_ci, "s_d": s_d, "w_glu": w_glu, "bgT": bgT,
                     "w_out": w_out, "wnT": wnT, "fnw": fnw, "idxG": idxG})
    return maps


_CACHE = {}


def kernel(**inputs):
    if "nc" not in _CACHE:
        _CACHE["nc"] = Builder().build()
    nc = _CACHE["nc"]
    maps = make_in_maps(inputs)
    res = run_bass_kernel_spmd(nc, maps, core_ids=list(range(NCORE)))
    y_prompt = np.zeros((1, 8 * NP_, D), np.float32)
    y_sample = np.zeros((4, 8 * NS_, D), np.float32)
    for c in range(NCORE):
        o = np.asarray(res.results[c]["out_loc"], np.float32)
        y_prompt[0, c * NP_:(c + 1) * NP_] = o[0:NP_]
        for j in range(4):
            y_sample[j, c * NS_:(c + 1) * NS_] = o[NP_ + j * NS_:NP_ + (j + 1) * NS_]
    return (y_prompt, y_sample)
```
